# Optimizing a Trainium2 kernel written in Bass

```python
import math
import jax, jax.numpy as jnp
from jax import lax
import numpy as np

D_MODEL = 1024
BATCH = 2
SEQ = 8192
DEPTH = 2

N_MIXERS = 2
N_A = (DEPTH + 1) // 2
N_B = DEPTH // 2

D_RNN = 1280
LRU_BLOCKS = 16
LRU_BLOCK_W = D_RNN // LRU_BLOCKS
LRU_C = 8.0
LRU_CONV_W = 4

ATTN_GROUPS = ((128, 1), (512, 4), (2048, 16))
N_GROUPS = len(ATTN_GROUPS)
N_HEADS = 16
HEAD_DIM = D_MODEL // N_HEADS
D_ATTN = N_HEADS * HEAD_DIM

D_FF = 2816
FF_CONV_W = 3

LN_EPS = 1e-5
DN_ALPHA = (2 * DEPTH) ** 0.25
DN_BETA = (8 * DEPTH) ** -0.25
NEG = -1e30

kernel_name = "hybrid_rglru_dilated_attn_encoder"


def layer_norm(z, g, b):
    zf = z.astype(jnp.float32)
    mu = jnp.mean(zf, axis=-1, keepdims=True)
    var = jnp.mean(jnp.square(zf - mu), axis=-1, keepdims=True)
    return ((zf - mu) * lax.rsqrt(var + LN_EPS) * g + b).astype(z.dtype)


def depthwise_conv(z, w, b, left):
    K = w.shape[0]
    S = z.shape[1]
    zp = jnp.pad(z, ((0, 0), (left, K - 1 - left), (0, 0)))
    y = zp[:, 0:S] * w[0]
    for k in range(1, K):
        y = y + zp[:, k:k + S] * w[k]
    return y + b


def lru_combine(left, right):
    a1, b1 = left
    a2, b2 = right
    return a1 * a2, a2 * b1 + b2


def rglru_scan(u, ub, w_a, b_a, w_x, b_x, lam, reverse):
    B, S, _ = u.shape
    r = jax.nn.sigmoid(jnp.einsum('bsnc,ncd->bsnd', ub, w_a).reshape(B, S, D_RNN) + b_a).astype(jnp.float32)
    i = jax.nn.sigmoid(jnp.einsum('bsnc,ncd->bsnd', ub, w_x).reshape(B, S, D_RNN) + b_x).astype(jnp.float32)
    log_a = -LRU_C * jax.nn.softplus(-lam.astype(jnp.float32)) * r
    a = jnp.exp(log_a)
    inp = jnp.sqrt(-jnp.expm1(2.0 * log_a)) * (i * u.astype(jnp.float32))
    _, h = lax.associative_scan(lru_combine, (a, inp), axis=1, reverse=reverse)
    return h


def rglru_mixer(x, w_in, conv_w, conv_b, w_a, b_a, w_x, b_x, lam, w_out):
    B, S, _ = x.shape
    gate, u = jnp.split(x @ w_in, 2, axis=-1)
    u = depthwise_conv(u, conv_w, conv_b, LRU_CONV_W // 2)
    ub = u.reshape(B, S, LRU_BLOCKS, LRU_BLOCK_W)
    h_fwd = rglru_scan(u, ub, w_a[0], b_a[0], w_x[0], b_x[0], lam[0], False)
    h_bwd = rglru_scan(u, ub, w_a[1], b_a[1], w_x[1], b_x[1], lam[1], True)
    y = jax.nn.gelu(gate) * (h_fwd + h_bwd).astype(x.dtype)
    return y @ w_out


def alibi_slopes():
    return jnp.exp2(-8.0 * jnp.arange(1, N_HEADS + 1, dtype=jnp.float32) / N_HEADS)


def dilated_band_attention(q, k, v, window, dil, slopes):
    B, S, H, Dh = q.shape
    half = window // (2 * dil)
    blk = half
    L = S // dil
    nb = -(-L // blk)
    Lp = nb * blk

    def to_blocks(z):
        z = z.reshape(B, L, dil, H, Dh).transpose(0, 2, 1, 3, 4)
        z = jnp.pad(z, ((0, 0), (0, 0), (0, Lp - L), (0, 0), (0, 0)))
        return z.reshape(B, dil, nb, blk, H, Dh)

    def neighbours(z):
        zp = jnp.pad(z, ((0, 0), (0, 0), (1, 1), (0, 0), (0, 0), (0, 0)))
        return jnp.concatenate([zp[:, :, :-2], zp[:, :, 1:-1], zp[:, :, 2:]], axis=3)

    qb = to_blocks(q)
    kb = neighbours(to_blocks(k))
    vb = neighbours(to_blocks(v))

    rel = jnp.arange(3 * blk)[None, :] - blk - jnp.arange(blk)[:, None]
    kidx = (jnp.arange(nb)[:, None] - 1) * blk + jnp.arange(3 * blk)[None, :]
    valid = (jnp.abs(rel) <= half)[None] & ((kidx >= 0) & (kidx < L))[:, None, :]
    bias = -slopes[:, None, None] * (jnp.abs(rel) * dil).astype(jnp.float32)[None]

    s = jnp.einsum('brnihd,brnjhd->brnhij', qb, kb,
                   preferred_element_type=jnp.float32) * (Dh ** -0.5) + bias
    s = jnp.where(valid[:, None], s, NEG)
    m = jnp.max(s, axis=-1, keepdims=True)
    p = jnp.exp(s - m)
    den = jnp.sum(p, axis=-1, keepdims=True)
    o = jnp.einsum('brnhij,brnjhd->brnihd', p, vb.astype(jnp.float32))
    o = o * jnp.swapaxes(1.0 / den, 3, 4)
    lse = (m + jnp.log(den))[..., 0]

    o = o.reshape(B, dil, Lp, H, Dh)[:, :, :L].transpose(0, 2, 1, 3, 4).reshape(B, S, H, Dh)
    lse = jnp.swapaxes(lse, 3, 4).reshape(B, dil, Lp, H)[:, :, :L].transpose(0, 2, 1, 3).reshape(B, S, H)
    return o, lse


def dilated_attention_mixer(x, w_qkv, w_o):
    B, S, _ = x.shape
    qkv = (x @ w_qkv).reshape(B, S, N_GROUPS, 3, N_HEADS, HEAD_DIM)
    slopes = alibi_slopes()
    outs, lses = [], []
    for gi, (window, dil) in enumerate(ATTN_GROUPS):
        o, lse = dilated_band_attention(qkv[:, :, gi, 0], qkv[:, :, gi, 1], qkv[:, :, gi, 2],
                                        window, dil, slopes)
        outs.append(o)
        lses.append(lse)
    wts = jax.nn.softmax(jnp.stack(lses), axis=0)
    o = jnp.einsum('gbsh,gbshd->bshd', wts, jnp.stack(outs))
    return o.reshape(B, S, D_ATTN).astype(x.dtype) @ w_o


def conv_ffn(x, w_up, conv_w, conv_b, w_down):
    v, g = jnp.split(x @ w_up, 2, axis=-1)
    g = depthwise_conv(g, conv_w, conv_b, FF_CONV_W // 2)
    return (jax.nn.gelu(g) * v) @ w_down


def setup_inputs(seed: int = 0) -> dict:
    key = jax.random.key(seed)
    ks = jax.random.split(key, 20)
    f32 = jnp.float32
    nrm = lambda k, shape, scale: jax.random.normal(k, shape, f32) * scale
    u = jax.random.uniform(ks[9], (N_A, 2, D_RNN), f32, 0.9, 0.999)
    a0 = u ** (1.0 / LRU_C)
    return {
        "x": jax.random.normal(ks[0], (BATCH, SEQ, D_MODEL), f32),
        "ln_g": 1.0 + nrm(ks[1], (DEPTH, 2, D_MODEL), 0.02),
        "ln_b": nrm(ks[2], (DEPTH, 2, D_MODEL), 0.02),
        "rg_w_in": nrm(ks[3], (N_A, D_MODEL, 2 * D_RNN), D_MODEL ** -0.5),
        "rg_conv_w": nrm(ks[4], (N_A, LRU_CONV_W, D_RNN), LRU_CONV_W ** -0.5),
        "rg_conv_b": nrm(ks[5], (N_A, D_RNN), 0.01),
        "rg_w_a": nrm(ks[6], (N_A, 2, LRU_BLOCKS, LRU_BLOCK_W, LRU_BLOCK_W), LRU_BLOCK_W ** -0.5),
        "rg_b_a": nrm(ks[7], (N_A, 2, D_RNN), 0.01),
        "rg_w_x": nrm(ks[8], (N_A, 2, LRU_BLOCKS, LRU_BLOCK_W, LRU_BLOCK_W), LRU_BLOCK_W ** -0.5),
        "rg_b_x": nrm(ks[10], (N_A, 2, D_RNN), 0.01),
        "rg_lam": jnp.log(a0) - jnp.log1p(-a0),
        "rg_w_out": nrm(ks[11], (N_A, D_RNN, D_MODEL), DN_BETA * D_RNN ** -0.5),
        "at_w_qkv": nrm(ks[12], (N_B, D_MODEL, N_GROUPS * 3 * D_ATTN), D_MODEL ** -0.5),
        "at_w_o": nrm(ks[13], (N_B, D_ATTN, D_MODEL), DN_BETA * D_ATTN ** -0.5),
        "ff_w_up": nrm(ks[14], (DEPTH, D_MODEL, 2 * D_FF), D_MODEL ** -0.5),
        "ff_conv_w": nrm(ks[15], (DEPTH, FF_CONV_W, D_FF), FF_CONV_W ** -0.5),
        "ff_conv_b": nrm(ks[16], (DEPTH, D_FF), 0.01),
        "ff_w_down": nrm(ks[17], (DEPTH, D_FF, D_MODEL), DN_BETA * D_FF ** -0.5),
    }


def reference(x, ln_g, ln_b, rg_w_in, rg_conv_w, rg_conv_b, rg_w_a, rg_b_a, rg_w_x, rg_b_x,
              rg_lam, rg_w_out, at_w_qkv, at_w_o, ff_w_up, ff_conv_w, ff_conv_b, ff_w_down):
    for i in range(DEPTH):
        j = i // N_MIXERS
        if i % N_MIXERS == 0:
            mix = rglru_mixer(x, rg_w_in[j], rg_conv_w[j], rg_conv_b[j], rg_w_a[j], rg_b_a[j],
                              rg_w_x[j], rg_b_x[j], rg_lam[j], rg_w_out[j])
        else:
            mix = dilated_attention_mixer(x, at_w_qkv[j], at_w_o[j])
        x = layer_norm(DN_ALPHA * x + mix, ln_g[i, 0], ln_b[i, 0])
        ffn = conv_ffn(x, ff_w_up[i], ff_conv_w[i], ff_conv_b[i], ff_w_down[i])
        x = layer_norm(DN_ALPHA * x + ffn, ln_g[i, 1], ln_b[i, 1])
    return x
```

```python
from contextlib import ExitStack
import numpy as np
import ml_dtypes
import concourse.bass as bass
import concourse.mybir as mybir
from concourse.ap import AP
from concourse.bass_utils import run_bass_kernel_spmd

F32 = mybir.dt.float32
BF16 = mybir.dt.bfloat16
AF = mybir.ActivationFunctionType
ALU = mybir.AluOpType
AX = mybir.AxisListType
NPBF = ml_dtypes.bfloat16

D = 1024
SEQ = 8192
NB = 2
NT = NB * SEQ
DR = 1280
DFF = 2816
NCORE = 8
TC = NT // NCORE
DN_ALPHA = 4 ** 0.25
LN_EPS = 1e-5
NEGBIG = -1.0e30


class Buf:
    __slots__ = ("w", "r")

    def __init__(self):
        self.w = None
        self.r = {}


class Prog:
    ENGS = ("pe", "act", "dve", "pool", "sp")

    def __init__(self, nc, stack, dma_ring=20):
        self.nc = nc
        self.stack = stack
        self.ops = {e: [] for e in self.ENGS}
        self.sems = {}
        self.cnt = {e: 0 for e in self.ENGS}
        self.seen = {e: {} for e in self.ENGS}
        for e in ("pe", "act", "dve", "pool"):
            self.sems[e] = stack.enter_context(nc.semaphore("s_" + e))
        self.ring = {}
        self.dma_n = {}
        self.dma_ring = dma_ring
        for q in ("sp", "pool", "act"):
            self.ring[q] = []
            for i in range(dma_ring):
                k = "d_%s_%d" % (q, i)
                self.sems[k] = stack.enter_context(nc.semaphore(k))
                self.ring[q].append(k)
            self.dma_n[q] = 0
        self.out_tokens = []

    def sbuf(self, name, shape, dt):
        return self.stack.enter_context(self.nc.sbuf_tensor(name, list(shape), dt))

    def psum(self, name, shape, dt=F32):
        return self.stack.enter_context(self.nc.psum_tensor(name, list(shape), dt))

    def buf(self):
        return Buf()

    def bufs(self, n):
        return [Buf() for _ in range(n)]

    def _waits(self, eng, reads, writes, same_ok):
        need = {}

        def add(tok):
            if tok is None:
                return
            k, v = tok
            if same_ok and k == eng:
                return
            if need.get(k, 0) < v:
                need[k] = v
        for b in reads:
            add(b.w)
        for b in writes:
            add(b.w)
            for k, v in b.r.items():
                add((k, v))
        out = []
        seen = self.seen[eng]
        for k, v in need.items():
            if seen.get(k, 0) >= v:
                continue
            seen[k] = v
            out.append((k, v))
        return out

    def _mark(self, tok, reads, writes):
        k, v = tok
        for b in reads:
            if b.r.get(k, 0) < v:
                b.r[k] = v
        for b in writes:
            b.w = tok
            b.r = {}

    def op(self, eng, fn, reads=(), writes=(), inc=True):
        waits = self._waits(eng, reads, writes, eng == "pe")
        if inc:
            self.cnt[eng] += 1
            tok = (eng, self.cnt[eng])
        else:
            tok = (eng, self.cnt[eng] + 1)
        sems = self.sems
        mysem = sems[eng]

        def run(e, waits=waits, fn=fn, inc=inc):
            for k, v in waits:
                e.wait_ge(sems[k], v)
            ins = fn(e)
            if inc:
                ins.then_inc(mysem, 1)
        self.ops[eng].append(run)
        self._mark(tok, reads, writes)
        return tok

    def dma(self, q, out, in_, reads=(), writes=(), is_output=False):
        waits = self._waits(q, reads, writes, False)
        n = self.dma_n[q]
        self.dma_n[q] += 1
        slot = self.ring[q][n % self.dma_ring]
        uses = n // self.dma_ring
        sems = self.sems
        if uses > 0:
            if self.seen[q].get(slot, 0) < 16 * uses:
                self.seen[q][slot] = 16 * uses
                waits = waits + [(slot, 16 * uses)]
        tok = (slot, 16 * (uses + 1))

        def run(e, waits=waits, out=out, in_=in_, slot=slot):
            for k, v in waits:
                e.wait_ge(sems[k], v)
            e.dma_start(out=out, in_=in_).then_inc(sems[slot], 16)
        self.ops[q].append(run)
        self._mark(tok, reads, writes)
        if is_output:
            self.out_tokens.append(tok)
        return tok

    def finish(self):
        final = {}
        for k, v in self.out_tokens:
            if final.get(k, 0) < v:
                final[k] = v
        sems = self.sems

        def fin(e):
            for k, v in final.items():
                e.wait_ge(sems[k], v)
        self.ops["sp"].append(fin)
        ops = self.ops
        with self.nc.Block() as block:
            @block.sync
            def _(e):
                for f in ops["sp"]:
                    f(e)

            @block.tensor
            def _(e):
                for f in ops["pe"]:
                    f(e)

            @block.scalar
            def _(e):
                for f in ops["act"]:
                    f(e)

            @block.vector
            def _(e):
                for f in ops["dve"]:
                    f(e)

            @block.gpsimd
            def _(e):
                for f in ops["pool"]:
                    f(e)


def rev(ap):
    (ps, pc), (s, n) = ap.ap
    return AP(ap.tensor, ap.offset + (n - 1) * s, [[ps, pc], [-s, n]])


def new_nc():
    return bass.Bass("TRN2", target_bir_lowering=False)


def build_cast(F, CH=4096):
    nc = new_nc()
    src = nc.dram_tensor("src", [128, F], F32, kind="ExternalInput").ap()
    dst = nc.dram_tensor("dst", [128, F], BF16, kind="ExternalOutput").ap()
    n = F // CH
    with ExitStack() as st:
        P = Prog(nc, st)
        NBUF = 3
        a = [P.sbuf("a%d" % i, [128, CH], F32) for i in range(NBUF)]
        b = [P.sbuf("b%d" % i, [128, CH], BF16) for i in range(NBUF)]
        ba = P.bufs(NBUF)
        bb = P.bufs(NBUF)
        for i in range(n):
            j = i % NBUF
            P.dma("sp", a[j][:], src[:, i * CH:(i + 1) * CH], writes=[ba[j]])
            eng = ("dve", "pool")[i % 2]
            P.op(eng, lambda e, j=j: e.tensor_copy(out=b[j][:], in_=a[j][:]), reads=[ba[j]], writes=[bb[j]])
            P.dma("act", dst[:, i * CH:(i + 1) * CH], b[j][:], reads=[bb[j]], is_output=True)
        P.finish()
    return nc


FRONT = True
KMOD = 1000
KOFF = 0


def build_gemm(K, M, N, KMM=None):
    nc = new_nc()
    KC = K // 128
    KMM = KMM or KC
    A = nc.dram_tensor("A", [K, M], BF16, kind="ExternalInput").ap()
    B = nc.dram_tensor("B", [K, N], BF16, kind="ExternalInput").ap()
    C32 = nc.dram_tensor("C32", [M, N], F32, kind="ExternalOutput").ap()
    C16 = nc.dram_tensor("C16", [M, N], BF16, kind="ExternalOutput").ap()
    Av = A.rearrange("(c p) m -> p c m", p=128)
    Bv = B.rearrange("(c p) n -> p c n", p=128)
    SL = 512 if M % 512 == 0 else 128
    NS = M // SL
    MT = SL // 128
    NTL = N // 512
    with ExitStack() as st:
        P = Prog(nc, st)
        Btl = [P.sbuf("Bt%d" % k, [128, N], BF16) for k in range(KC)]
        bB = P.bufs(KC)
        NA = 2
        At = [[P.sbuf("At%d_%d" % (i, k), [128, SL], BF16) for k in range(KC)] for i in range(NA)]
        bA = P.bufs(NA)
        NO = 3
        o32 = [P.sbuf("o32_%d" % i, [128, N], F32) for i in range(NO)]
        o16 = [P.sbuf("o16_%d" % i, [128, N], BF16) for i in range(NO)]
        b32 = P.bufs(NO)
        b16 = P.bufs(NO)
        ps = [P.psum("ps%d" % i, [128, 512]) for i in range(8)]
        bps = P.bufs(8)
        for k in range(KC):
            P.dma("sp", Btl[k][:], Bv[:, k, :], writes=[bB[k]])
        pi = 0
        oi = 0
        def load_A(s):
            jj = s % NA
            for k in range(KC):
                P.dma("sp", At[jj][k][:], Av[:, k, s * SL:(s + 1) * SL], writes=[bA[jj]])
        load_A(0)
        for s in range(NS):
            ja = s % NA
            if s + 1 < NS:
                load_A(s + 1)
            for mt in range(MT):
                jo = oi % NO
                oi += 1
                for nt in range(NTL):
                    cs = slice(nt * 512, (nt + 1) * 512)
                    for j in range(KC // 2):
                        p = pi % 8
                        pi += 1
                        for kk in range(2):
                            k = 2 * j + kk
                            P.op("pe", lambda e, p=p, ja=ja, k=k, kk=kk, mt=mt, cs=cs: e.matmul(
                                ps[p][:], lhsT=At[ja][k][:, mt * 128:(mt + 1) * 128], rhs=Btl[k][:, cs],
                                start=(kk == 0), stop=(kk == 1)),
                                reads=[bA[ja], bB[k]], writes=[bps[p]], inc=(kk == 1))
                        if j == 0:
                            P.op("act", lambda e, p=p, jo=jo, cs=cs: e.activation(
                                out=o32[jo][:, cs], in_=ps[p][:], func=AF.Copy), reads=[bps[p]], writes=[b32[jo]])
                        else:
                            P.op("dve", lambda e, p=p, jo=jo, cs=cs: e.tensor_tensor(
                                out=o32[jo][:, cs], in0=o32[jo][:, cs], in1=ps[p][:], op=ALU.add),
                                reads=[bps[p], b32[jo]], writes=[b32[jo]])
                    P.op("pool", lambda e, jo=jo, cs=cs: e.tensor_copy(out=o16[jo][:, cs], in_=o32[jo][:, cs]),
                         reads=[b32[jo]], writes=[b16[jo]])
                r0 = s * SL + mt * 128
                P.dma("act", C32[r0:r0 + 128, :], o32[jo][:], reads=[b32[jo]], is_output=True)
                P.dma("sp", C16[r0:r0 + 128, :], o16[jo][:], reads=[b16[jo]], is_output=True)
        P.finish()
    return nc


def build_ln(T):
    nc = new_nc()
    res = nc.dram_tensor("res", [T, D], F32, kind="ExternalInput").ap()
    mix = nc.dram_tensor("mix", [T, D], F32, kind="ExternalInput").ap()
    gb = nc.dram_tensor("gb", [128, 2, D], F32, kind="ExternalInput").ap()
    o32 = nc.dram_tensor("o32", [T, D], F32, kind="ExternalOutput").ap()
    o16 = nc.dram_tensor("o16", [T, D], BF16, kind="ExternalOutput").ap()
    n = T // 128
    with ExitStack() as st:
        P = Prog(nc, st)
        gbt = P.sbuf("gbt", [128, 2, D], F32)
        bgb = P.buf()
        P.dma("sp", gbt[:], gb, writes=[bgb])
        NBF = 2
        rt = [P.sbuf("rt%d" % i, [128, D], F32) for i in range(NBF)]
        mt = [P.sbuf("mt%d" % i, [128, D], F32) for i in range(NBF)]
        zt = [P.sbuf("zt%d" % i, [128, D], F32) for i in range(NBF)]
        xt = [P.sbuf("xt%d" % i, [128, D], F32) for i in range(NBF)]
        xb = [P.sbuf("xb%d" % i, [128, D], BF16) for i in range(NBF)]
        s6 = [P.sbuf("s6%d" % i, [128, 12], F32) for i in range(NBF)]
        mv = [P.sbuf("mv%d" % i, [128, 4], F32) for i in range(NBF)]
        brt, bmt, bzt, bxt, bxb, bs6, bmv = (P.bufs(NBF) for _ in range(7))
        for i in range(n):
            j = i % NBF
            rows = slice(i * 128, (i + 1) * 128)
            P.dma("sp", rt[j][:], res[rows, :], writes=[brt[j]])
            P.dma("sp", mt[j][:], mix[rows, :], writes=[bmt[j]])
            P.op("dve", lambda e, j=j: e.scalar_tensor_tensor(
                out=zt[j][:], in0=rt[j][:], scalar=float(DN_ALPHA), in1=mt[j][:], op0=ALU.mult, op1=ALU.add),
                reads=[brt[j], bmt[j]], writes=[bzt[j]])
            P.op("dve", lambda e, j=j: e.bn_stats(out=s6[j][:, 0:6], in_=zt[j][:, 0:512]), reads=[bzt[j]], writes=[bs6[j]])
            P.op("dve", lambda e, j=j: e.bn_stats(out=s6[j][:, 6:12], in_=zt[j][:, 512:1024]), reads=[bzt[j]], writes=[bs6[j]])
            P.op("dve", lambda e, j=j: e.bn_aggr(out=mv[j][:, 0:2], in_=s6[j][:]), reads=[bs6[j]], writes=[bmv[j]])
            P.op("act", lambda e, j=j: e.activation(out=mv[j][:, 2:3], in_=mv[j][:, 1:2], func=AF.Ln, bias=float(LN_EPS)),
                 reads=[bmv[j]], writes=[bmv[j]])
            P.op("act", lambda e, j=j: e.activation(out=mv[j][:, 3:4], in_=mv[j][:, 2:3], func=AF.Exp, scale=-0.5),
                 reads=[bmv[j]], writes=[bmv[j]])
            P.op("dve", lambda e, j=j: e.tensor_scalar(
                out=zt[j][:], in0=zt[j][:], scalar1=mv[j][:, 0:1], scalar2=mv[j][:, 3:4], op0=ALU.subtract, op1=ALU.mult),
                reads=[bzt[j], bmv[j]], writes=[bzt[j]])
            P.op("pool", lambda e, j=j: e.tensor_tensor(out=xt[j][:], in0=zt[j][:], in1=gbt[:, 0, :], op=ALU.mult),
                 reads=[bzt[j], bgb], writes=[bxt[j]])
            P.op("pool", lambda e, j=j: e.tensor_tensor(out=xt[j][:], in0=xt[j][:], in1=gbt[:, 1, :], op=ALU.add),
                 reads=[bxt[j], bgb], writes=[bxt[j]])
            P.op("act", lambda e, j=j: e.activation(out=xb[j][:], in_=xt[j][:], func=AF.Copy), reads=[bxt[j]], writes=[bxb[j]])
            P.dma("sp", o32[rows, :], xt[j][:], reads=[bxt[j]], is_output=True)
            P.dma("sp", o16[rows, :], xb[j][:], reads=[bxb[j]], is_output=True)
        P.finish()
    return nc


def build_ffnmid(T, C=DFF):
    nc = new_nc()
    CT = C // 128
    v = nc.dram_tensor("v", [C, T], F32, kind="ExternalInput").ap()
    g = nc.dram_tensor("g", [C, T + 2], F32, kind="ExternalInput").ap()
    cw = nc.dram_tensor("cw", [128, CT, 4], F32, kind="ExternalInput").ap()
    h = nc.dram_tensor("h", [C, T], BF16, kind="ExternalOutput").ap()
    NTL = T // 512
    with ExitStack() as st:
        P = Prog(nc, st)
        cwt = P.sbuf("cwt", [128, CT, 4], F32)
        bcw = P.buf()
        P.dma("sp", cwt[:], cw, writes=[bcw])
        NBF = 3
        gt = [P.sbuf("gt%d" % i, [128, 514], F32) for i in range(NBF)]
        vt = [P.sbuf("vt%d" % i, [128, 512], F32) for i in range(NBF)]
        at = [P.sbuf("at%d" % i, [128, 512], F32) for i in range(NBF)]
        ht = [P.sbuf("ht%d" % i, [128, 512], BF16) for i in range(NBF)]
        bg, bv, ba, bh = (P.bufs(NBF) for _ in range(4))
        it = 0
        for c in range(CT):
            rows = slice(c * 128, (c + 1) * 128)
            for t in range(NTL):
                j = it % NBF
                it += 1
                P.dma("sp", gt[j][:], g[rows, t * 512:t * 512 + 514], writes=[bg[j]])
                P.dma("sp", vt[j][:], v[rows, t * 512:(t + 1) * 512], writes=[bv[j]])
                P.op("dve", lambda e, j=j, c=c: e.tensor_scalar(
                    out=at[j][:], in0=gt[j][:, 1:513], scalar1=cwt[:, c, 1:2], scalar2=cwt[:, c, 3:4],
                    op0=ALU.mult, op1=ALU.add), reads=[bg[j], bcw], writes=[ba[j]])
                P.op("dve", lambda e, j=j, c=c: e.scalar_tensor_tensor(
                    out=at[j][:], in0=gt[j][:, 0:512], scalar=cwt[:, c, 0:1], in1=at[j][:],
                    op0=ALU.mult, op1=ALU.add), reads=[bg[j], bcw, ba[j]], writes=[ba[j]])
                P.op("dve", lambda e, j=j, c=c: e.scalar_tensor_tensor(
                    out=at[j][:], in0=gt[j][:, 2:514], scalar=cwt[:, c, 2:3], in1=at[j][:],
                    op0=ALU.mult, op1=ALU.add), reads=[bg[j], bcw, ba[j]], writes=[ba[j]])
                P.op("act", lambda e, j=j: e.activation(out=at[j][:], in_=at[j][:], func=AF.Gelu_apprx_tanh),
                     reads=[ba[j]], writes=[ba[j]])
                P.op("pool", lambda e, j=j: e.tensor_tensor(out=ht[j][:], in0=at[j][:], in1=vt[j][:], op=ALU.mult),
                     reads=[ba[j], bv[j]], writes=[bh[j]])
                P.dma("act", h[rows, t * 512:(t + 1) * 512], ht[j][:], reads=[bh[j]], is_output=True)
        P.finish()
    return nc


def build_scan(NU, S):
    nc = new_nc()
    CW = 80
    up = nc.dram_tensor("up", [NU, CW, S], F32, kind="ExternalInput").ap()
    gate = nc.dram_tensor("gate", [NU, CW, S], F32, kind="ExternalInput").ap()
    pp = nc.dram_tensor("pp", [NU, CW, 16], F32, kind="ExternalInput").ap()
    gw = nc.dram_tensor("gw", [NU, CW, 4, CW], BF16, kind="ExternalInput").ap()
    y = nc.dram_tensor("y", [NU, CW, S], BF16, kind="ExternalOutput").ap()
    HS = min(S, 4096)
    NH = S // HS
    CH = min(S, 2048)
    with ExitStack() as st:
        P = Prog(nc, st)
        upt = P.sbuf("upt", [CW, S], F32)
        ut = P.sbuf("ut", [CW, S], F32)
        ubt = P.sbuf("ubt", [CW, S], BF16)
        hft = P.sbuf("hft", [CW, S], F32)
        hbt = upt
        TA = P.sbuf("TA", [CW, HS], F32)
        TX = P.sbuf("TX", [CW, HS], F32)
        AA = P.sbuf("AA", [CW, HS], F32)
        ppt = P.sbuf("ppt", [CW, 16], F32)
        cc = P.sbuf("cc", [CW, 8], F32)
        gwt = P.sbuf("gwt", [CW, 4, CW], BF16)
        NG = 2
        gt = [P.sbuf("gt%d" % i, [CW, CH], F32) for i in range(NG)]
        yt = [P.sbuf("yt%d" % i, [CW, CH], BF16) for i in range(NG)]
        bgt = P.bufs(NG)
        byt = P.bufs(NG)
        nchs = S // 512
        bup = P.bufs(nchs)
        bu = P.bufs(nchs)
        bub = P.bufs(nchs)
        bhf = P.bufs(nchs)
        bTA = P.bufs(HS // 512)
        bTX = P.bufs(HS // 512)
        bAA = P.bufs(HS // 512)
        bpp, bcc, bgw = P.buf(), P.buf(), P.buf()
        ps = [P.psum("ps%d" % i, [CW, 512]) for i in range(4)]
        bps = P.bufs(4)
        pi = 0

        def chs(lo, hi):
            return list(range(lo // 512, (hi + 511) // 512))

        for u in range(NU):
            P.dma("sp", ppt[:], pp[u], writes=[bpp])
            P.dma("sp", gwt[:], gw[u], writes=[bgw])
            for c0 in range(0, S, CH):
                P.dma("sp", upt[:, c0:c0 + CH], up[u, :, c0:c0 + CH], writes=[bup[i] for i in chs(c0, c0 + CH)])
            P.op("act", lambda e: e.activation(out=cc[:, 0:2], in_=ppt[:, 9:11], func=AF.Exp, scale=-1.0),
                 reads=[bpp], writes=[bcc])
            P.op("act", lambda e: e.activation(out=cc[:, 0:2], in_=cc[:, 0:2], func=AF.Ln, bias=1.0),
                 reads=[bcc], writes=[bcc])
            P.op("dve", lambda e: e.tensor_scalar(out=cc[:, 2:4], in0=cc[:, 0:2], scalar1=-4.0, scalar2=None, op0=ALU.mult),
                 reads=[bcc], writes=[bcc])
            P.op("dve", lambda e: e.tensor_scalar(out=cc[:, 4:6], in0=cc[:, 0:2], scalar1=-8.0, scalar2=None, op0=ALU.mult),
                 reads=[bcc], writes=[bcc])
            for c0 in range(0, S, CH):
                c1_ = c0 + CH
                wr = [bu[i] for i in chs(c0, c1_)]
                rd = [bup[i] for i in chs(max(c0 - 2, 0), min(c1_ + 1, S))]
                P.op("dve", lambda e, c0=c0, c1_=c1_: e.tensor_scalar(
                    out=ut[:, c0:c1_], in0=upt[:, c0:c1_], scalar1=ppt[:, 2:3], scalar2=ppt[:, 4:5],
                    op0=ALU.mult, op1=ALU.add), reads=rd + [bpp], writes=wr)
                for tap, sh in ((0, -2), (1, -1), (3, 1)):
                    lo = max(c0, -sh) if sh < 0 else c0
                    hi = c1_ if sh < 0 else min(c1_, S - sh)
                    P.op("dve", lambda e, lo=lo, hi=hi, sh=sh, tap=tap: e.scalar_tensor_tensor(
                        out=ut[:, lo:hi], in0=upt[:, lo + sh:hi + sh], scalar=ppt[:, tap:tap + 1], in1=ut[:, lo:hi],
                        op0=ALU.mult, op1=ALU.add), reads=rd + [bpp] + wr, writes=wr)
                P.op("act", lambda e, c0=c0, c1_=c1_: e.activation(out=ubt[:, c0:c1_], in_=ut[:, c0:c1_], func=AF.Copy),
                     reads=wr, writes=[bub[i] for i in chs(c0, c1_)])
            for d in range(2):
                hout = hft if d == 0 else hbt
                bho = bhf if d == 0 else bup
                halves = list(range(NH)) if d == 0 else list(range(NH - 1, -1, -1))
                for hi_, hh in enumerate(halves):
                    h0 = hh * HS
                    nck = HS // 512
                    for ck in range(nck):
                        g0 = h0 + ck * 512
                        gi = g0 // 512
                        for gsel, dst, bdst in ((0, TA, bTA), (1, TX, bTX)):
                            p = pi % 4
                            pi += 1
                            P.op("pe", lambda e, p=p, d=d, gsel=gsel, g0=g0: e.matmul(
                                ps[p][:], lhsT=gwt[:, 2 * d + gsel, :], rhs=ubt[:, g0:g0 + 512], start=True, stop=True),
                                reads=[bgw, bub[gi]], writes=[bps[p]])
                            bcol = (5 if gsel == 0 else 7) + d
                            P.op("dve", lambda e, bcol=bcol: e.tensor_scalar(
                                out=cc[:, 6:7], in0=ppt[:, bcol:bcol + 1], scalar1=0.5, scalar2=None, op0=ALU.mult),
                                reads=[bpp, bcc], writes=[bcc])
                            P.op("act", lambda e, p=p, dst=dst, ck=ck: e.activation(
                                out=dst[:, ck * 512:(ck + 1) * 512], in_=ps[p][:], func=AF.Tanh, scale=0.5, bias=cc[:, 6:7]),
                                reads=[bps[p], bcc], writes=[bdst[ck]])
                    P.op("act", lambda e, d=d: e.activation(out=AA[:], in_=TA[:], func=AF.Exp,
                                                            scale=cc[:, 2 + d:3 + d], bias=cc[:, 2 + d:3 + d]),
                         reads=bTA + [bcc], writes=bAA)
                    P.op("act", lambda e, d=d: e.activation(out=TA[:], in_=TA[:], func=AF.Exp,
                                                            scale=cc[:, 4 + d:5 + d], bias=cc[:, 4 + d:5 + d]),
                         reads=bTA + [bcc], writes=bTA)
                    P.op("act", lambda e: e.activation(out=TA[:], in_=TA[:], func=AF.Sqrt, scale=-1.0, bias=1.0),
                         reads=bTA, writes=bTA)
                    P.op("pool", lambda e: e.tensor_scalar(out=TX[:], in0=TX[:], scalar1=0.5, scalar2=0.5,
                                                           op0=ALU.mult, op1=ALU.add), reads=bTX, writes=bTX)
                    P.op("pool", lambda e, h0=h0: e.tensor_tensor(out=TX[:], in0=TX[:], in1=ut[:, h0:h0 + HS], op=ALU.mult),
                         reads=bTX + [bu[i] for i in chs(h0, h0 + HS)], writes=bTX)
                    P.op("pool", lambda e: e.tensor_tensor(out=TX[:], in0=TX[:], in1=TA[:], op=ALU.mult),
                         reads=bTX + bTA, writes=bTX)
                    wr = [bho[i] for i in chs(h0, h0 + HS)]
                    if d == 0:
                        init = 0.0 if hi_ == 0 else hout[:, h0 - 1:h0]
                        rdx = [] if hi_ == 0 else [bho[(h0 - 1) // 512]]
                        P.op("dve", lambda e, h0=h0, init=init, hout=hout: e.tensor_tensor_scan(
                            out=hout[:, h0:h0 + HS], data0=AA[:], data1=TX[:], initial=init, op0=ALU.mult, op1=ALU.add),
                            reads=bAA + bTX + rdx, writes=wr)
                    else:
                        init = 0.0 if hi_ == 0 else hout[:, h0 + HS:h0 + HS + 1]
                        rdx = [] if hi_ == 0 else [bho[(h0 + HS) // 512]]
                        P.op("dve", lambda e, h0=h0, init=init, hout=hout: e.tensor_tensor_scan(
                            out=rev(hout[:, h0:h0 + HS]), data0=rev(AA[:]), data1=rev(TX[:]), initial=init,
                            op0=ALU.mult, op1=ALU.add), reads=bAA + bTX + rdx, writes=wr)
            for ci, c0 in enumerate(range(0, S, CH)):
                j = ci % NG
                cs = chs(c0, c0 + CH)
                P.dma("sp", gt[j][:], gate[u, :, c0:c0 + CH], writes=[bgt[j]])
                P.op("act", lambda e, j=j: e.activation(out=gt[j][:], in_=gt[j][:], func=AF.Gelu_apprx_tanh),
                     reads=[bgt[j]], writes=[bgt[j]])
                P.op("dve", lambda e, c0=c0: e.tensor_tensor(out=hft[:, c0:c0 + CH], in0=hft[:, c0:c0 + CH],
                                                            in1=hbt[:, c0:c0 + CH], op=ALU.add),
                     reads=[bhf[i] for i in cs] + [bup[i] for i in cs], writes=[bhf[i] for i in cs])
                P.op("pool", lambda e, j=j, c0=c0: e.tensor_tensor(out=yt[j][:], in0=gt[j][:], in1=hft[:, c0:c0 + CH], op=ALU.mult),
                     reads=[bgt[j]] + [bhf[i] for i in cs], writes=[byt[j]])
                P.dma("act", y[u, :, c0:c0 + CH], yt[j][:], reads=[byt[j]], is_output=True)
        P.finish()
    return nc


def build_attn(gh_list, S):
    nc = new_nc()
    NGH = len(gh_list)
    NP = S // 128
    q = nc.dram_tensor("q", [NGH, 64, S], BF16, kind="ExternalInput").ap()
    k = nc.dram_tensor("k", [NGH, 64, S + 128], BF16, kind="ExternalInput").ap()
    v = nc.dram_tensor("v", [NGH, S + 128, 64], BF16, kind="ExternalInput").ap()
    tab = nc.dram_tensor("tab", [128, 3, 256], F32, kind="ExternalInput").ap()
    ident = nc.dram_tensor("ident", [128, 128], BF16, kind="ExternalInput").ap()
    coefs = nc.dram_tensor("coefs", [128, NGH], F32, kind="ExternalInput").ap()
    num = nc.dram_tensor("num", [NGH, S, 64], F32, kind="ExternalOutput").ap()
    mxo = nc.dram_tensor("mx", [NGH, 128, NP], F32, kind="ExternalOutput").ap()
    deno = nc.dram_tensor("den", [NGH, 128, NP], F32, kind="ExternalOutput").ap()
    with ExitStack() as st:
        P = Prog(nc, st)
        tabt = P.sbuf("tabt", [128, 3, 256], F32)
        idt = P.sbuf("idt", [128, 128], BF16)
        btab, bid = P.buf(), P.buf()
        P.dma("sp", tabt[:], tab, writes=[btab])
        P.dma("sp", idt[:], ident, writes=[bid])
        cft = P.sbuf("cft", [128, NGH], F32)
        P.dma("sp", cft[:], coefs, writes=[btab])
        NQ = 2
        qt = [P.sbuf("qt%d" % i, [64, S], BF16) for i in range(NQ)]
        kt = [P.sbuf("kt%d" % i, [64, S + 128], BF16) for i in range(NQ)]
        vt = [P.sbuf("vt%d" % i, [128, NP + 1, 64], BF16) for i in range(NQ)]
        on = [P.sbuf("on%d" % i, [128, NP, 64], F32) for i in range(NQ)]
        om = [P.sbuf("om%d" % i, [128, NP], F32) for i in range(NQ)]
        od = [P.sbuf("od%d" % i, [128, NP], F32) for i in range(NQ)]
        bq, bk, bv, bon, bom, bod = (P.bufs(NQ) for _ in range(6))
        NW = 3
        sS = [P.sbuf("sS%d" % i, [128, 256], F32) for i in range(NW)]
        pP = [P.sbuf("pP%d" % i, [128, 256], BF16) for i in range(NW)]
        pT = [P.sbuf("pT%d" % i, [128, 2, 128], BF16) for i in range(NW)]
        sm = [P.sbuf("sm%d" % i, [128, 2], F32) for i in range(NW)]
        bsS, bpP, bpT, bsm = (P.bufs(NW) for _ in range(4))
        psS = [P.psum("psS%d" % i, [128, 256]) for i in range(2)]
        psT = [P.psum("psT%d" % i, [128, 2, 128], BF16) for i in range(2)]
        psO = [P.psum("psO%d" % i, [128, 64]) for i in range(2)]
        bpsS, bpsT, bpsO = P.bufs(2), P.bufs(2), P.bufs(2)
        it = 0
        for gi, L in enumerate(gh_list):
            jq = gi % NQ
            P.dma("sp", qt[jq][:], q[gi], writes=[bq[jq]])
            P.dma("sp", kt[jq][:], k[gi], writes=[bk[jq]])
            P.dma("sp", vt[jq][:], v[gi].rearrange("(a p) d -> p a d", p=128), writes=[bv[jq]])
            ppc = L // 128
            for A in range(NP):
                a = A % ppc
                tv = 1 if a == 0 else (2 if a == ppc - 1 else 0)
                w = it % NW
                z = it % 2
                it += 1
                P.op("pe", lambda e, z=z, jq=jq, A=A: e.matmul(
                    psS[z][:], lhsT=qt[jq][:, A * 128:(A + 1) * 128], rhs=kt[jq][:, A * 128:A * 128 + 256],
                    start=True, stop=True), reads=[bq[jq], bk[jq]], writes=[bpsS[z]])
                P.op("dve", lambda e, w=w, z=z, tv=tv, gi=gi: e.scalar_tensor_tensor(
                    out=sS[w][:], in0=tabt[:, tv, :], scalar=cft[:, gi:gi + 1], in1=psS[z][:], op0=ALU.mult, op1=ALU.add),
                    reads=[btab, bpsS[z]], writes=[bsS[w]])
                P.op("dve", lambda e, w=w: e.reduce_max(out=sm[w][:, 0:1], in_=sS[w][:], axis=AX.X),
                     reads=[bsS[w]], writes=[bsm[w]])
                P.op("dve", lambda e, w=w: e.tensor_scalar(out=sm[w][:, 1:2], in0=sm[w][:, 0:1], scalar1=-0.125,
                                                           scalar2=None, op0=ALU.mult), reads=[bsm[w]], writes=[bsm[w]])
                P.op("dve", lambda e, w=w, jq=jq, A=A: e.tensor_scalar(out=om[jq][:, A:A + 1], in0=sm[w][:, 0:1],
                                                                        scalar1=0.125, scalar2=None, op0=ALU.mult),
                     reads=[bsm[w]], writes=[bom[jq]])
                P.op("act", lambda e, w=w, jq=jq, A=A: e.activation(
                    out=pP[w][:], in_=sS[w][:], func=AF.Exp, scale=0.125, bias=sm[w][:, 1:2], accum_out=od[jq][:, A:A + 1]),
                    reads=[bsS[w], bsm[w]], writes=[bpP[w], bod[jq]])
                for hh in range(2):
                    P.op("pe", lambda e, z=z, w=w, hh=hh: e.transpose(
                        psT[z][:, hh, :], pP[w][:, hh * 128:(hh + 1) * 128], idt[:]),
                        reads=[bpP[w], bid], writes=[bpsT[z]])
                P.op("act", lambda e, z=z, w=w: e.activation(out=pT[w][:], in_=psT[z][:], func=AF.Copy),
                     reads=[bpsT[z]], writes=[bpT[w]])
                for hh in range(2):
                    P.op("pe", lambda e, z=z, w=w, hh=hh, jq=jq, A=A: e.matmul(
                        psO[z][:], lhsT=pT[w][:, hh, :], rhs=vt[jq][:, A + hh, :], start=(hh == 0), stop=(hh == 1)),
                        reads=[bpT[w], bv[jq]], writes=[bpsO[z]], inc=(hh == 1))
                P.op("dve", lambda e, z=z, jq=jq, A=A: e.tensor_copy(out=on[jq][:, A, :], in_=psO[z][:]),
                     reads=[bpsO[z]], writes=[bon[jq]])
            P.dma("act", num[gi].rearrange("(a p) d -> p a d", p=128), on[jq][:], reads=[bon[jq]], is_output=True)
            P.dma("act", mxo[gi], om[jq][:], reads=[bom[jq]], is_output=True)
            P.dma("act", deno[gi], od[jq][:], reads=[bod[jq]], is_output=True)
        P.finish()
    return nc


def build_merge(T):
    nc = new_nc()
    num = nc.dram_tensor("num", [T, 3, D], F32, kind="ExternalInput").ap()
    mm = nc.dram_tensor("mm", [T, 48], F32, kind="ExternalInput").ap()
    den = nc.dram_tensor("den", [T, 48], F32, kind="ExternalInput").ap()
    o = nc.dram_tensor("o", [T, D], BF16, kind="ExternalOutput").ap()
    n = T // 128
    with ExitStack() as st:
        P = Prog(nc, st)
        NBF = 2
        nt = [P.sbuf("nt%d" % i, [128, 3, D], F32) for i in range(NBF)]
        mt = [P.sbuf("mt%d" % i, [128, 48], F32) for i in range(NBF)]
        dt_ = [P.sbuf("dt%d" % i, [128, 48], F32) for i in range(NBF)]
        wk = [P.sbuf("wk%d" % i, [128, 64], F32) for i in range(NBF)]
        ot = [P.sbuf("ot%d" % i, [128, D], F32) for i in range(NBF)]
        ob = [P.sbuf("ob%d" % i, [128, D], BF16) for i in range(NBF)]
        bn_, bm, bd, bw, bo, bob = (P.bufs(NBF) for _ in range(6))
        for i in range(n):
            j = i % NBF
            rows = slice(i * 128, (i + 1) * 128)
            P.dma("sp", nt[j][:], num[rows], writes=[bn_[j]])
            P.dma("sp", mt[j][:], mm[rows, :], writes=[bm[j]])
            P.dma("sp", dt_[j][:], den[rows, :], writes=[bd[j]])
            P.op("dve", lambda e, j=j: e.tensor_tensor(out=wk[j][:, 0:16], in0=mt[j][:, 0:16], in1=mt[j][:, 16:32], op=ALU.max),
                 reads=[bm[j]], writes=[bw[j]])
            P.op("dve", lambda e, j=j: e.tensor_tensor(out=wk[j][:, 0:16], in0=wk[j][:, 0:16], in1=mt[j][:, 32:48], op=ALU.max),
                 reads=[bm[j], bw[j]], writes=[bw[j]])
            for g in range(3):
                P.op("dve", lambda e, j=j, g=g: e.tensor_tensor(out=mt[j][:, g * 16:(g + 1) * 16], in0=mt[j][:, g * 16:(g + 1) * 16],
                                                                in1=wk[j][:, 0:16], op=ALU.subtract),
                     reads=[bm[j], bw[j]], writes=[bm[j]])
            P.op("act", lambda e, j=j: e.activation(out=mt[j][:], in_=mt[j][:], func=AF.Exp), reads=[bm[j]], writes=[bm[j]])
            P.op("dve", lambda e, j=j: e.tensor_tensor(out=dt_[j][:], in0=dt_[j][:], in1=mt[j][:], op=ALU.mult),
                 reads=[bm[j], bd[j]], writes=[bd[j]])
            P.op("dve", lambda e, j=j: e.tensor_tensor(out=wk[j][:, 16:32], in0=dt_[j][:, 0:16], in1=dt_[j][:, 16:32], op=ALU.add),
                 reads=[bd[j], bw[j]], writes=[bw[j]])
            P.op("dve", lambda e, j=j: e.tensor_tensor(out=wk[j][:, 16:32], in0=wk[j][:, 16:32], in1=dt_[j][:, 32:48], op=ALU.add),
                 reads=[bd[j], bw[j]], writes=[bw[j]])
            P.op("act", lambda e, j=j: e.activation(out=wk[j][:, 16:32], in_=wk[j][:, 16:32], func=AF.Ln), reads=[bw[j]], writes=[bw[j]])
            P.op("act", lambda e, j=j: e.activation(out=wk[j][:, 16:32], in_=wk[j][:, 16:32], func=AF.Exp, scale=-1.0),
                 reads=[bw[j]], writes=[bw[j]])
            for g in range(3):
                P.op("dve", lambda e, j=j, g=g: e.tensor_tensor(out=mt[j][:, g * 16:(g + 1) * 16], in0=mt[j][:, g * 16:(g + 1) * 16],
                                                                in1=wk[j][:, 16:32], op=ALU.mult),
                     reads=[bm[j], bw[j]], writes=[bm[j]])
            for h in range(16):
                cs = slice(h * 64, (h + 1) * 64)
                P.op("dve", lambda e, j=j, h=h, cs=cs: e.tensor_scalar(
                    out=ot[j][:, cs], in0=nt[j][:, 0, cs], scalar1=mt[j][:, h:h + 1], scalar2=None, op0=ALU.mult),
                    reads=[bn_[j], bm[j]], writes=[bo[j]])
                for g in (1, 2):
                    P.op("dve", lambda e, j=j, h=h, g=g, cs=cs: e.scalar_tensor_tensor(
                        out=ot[j][:, cs], in0=nt[j][:, g, cs], scalar=mt[j][:, g * 16 + h:g * 16 + h + 1], in1=ot[j][:, cs],
                        op0=ALU.mult, op1=ALU.add), reads=[bn_[j], bm[j], bo[j]], writes=[bo[j]])
            P.op("act", lambda e, j=j: e.activation(out=ob[j][:], in_=ot[j][:], func=AF.Copy), reads=[bo[j]], writes=[bob[j]])
            P.dma("act", o[rows, :], ob[j][:], reads=[bob[j]], is_output=True)
        P.finish()
    return nc


_CACHE = {}


def _get(key, fn):
    if key not in _CACHE:
        _CACHE[key] = fn()
    return _CACHE[key]


def _run(nc, in_maps):
    res = run_bass_kernel_spmd(nc, in_maps, core_ids=list(range(len(in_maps))))
    return res.results


def run_cast(flat):
    n = flat.size
    per = NCORE * 128 * 4096
    npad = (n + per - 1) // per * per
    buf = np.zeros(npad, np.float32)
    buf[:n] = flat
    F = npad // (NCORE * 128)
    buf = buf.reshape(NCORE, 128, F)
    nc = _get(("cast", F), lambda: build_cast(F))
    out = _run(nc, [{"src": buf[c]} for c in range(NCORE)])
    return np.concatenate([np.asarray(o["dst"]).reshape(-1) for o in out])[:n]


def run_gemm(Wb, actT):
    K, M = Wb.shape
    nc = _get(("gemm", K, M, TC), lambda: build_gemm(K, M, TC))
    Wb = np.ascontiguousarray(Wb)
    out = _run(nc, [{"A": Wb, "B": np.ascontiguousarray(actT[:, c * TC:(c + 1) * TC])} for c in range(NCORE)])
    C32 = np.concatenate([np.asarray(o["C32"]) for o in out], axis=1)
    C16 = np.concatenate([np.asarray(o["C16"]) for o in out], axis=1)
    return C32, C16


def pad_cols(W, M):
    out = np.zeros((W.shape[0], M), W.dtype)
    out[:, :W.shape[1]] = W
    return out


def pad_rows(W, K):
    out = np.zeros((K, W.shape[1]), W.dtype)
    out[:W.shape[0]] = W
    return out


def gemm_wide(Wb, actT, blk=3072):
    K, M = Wb.shape
    Mp = (M + blk - 1) // blk * blk
    Wp = pad_cols(Wb, Mp)
    c32, c16 = [], []
    for i in range(Mp // blk):
        a, b = run_gemm(Wp[:, i * blk:(i + 1) * blk], actT)
        c32.append(a)
        c16.append(b)
    return np.concatenate(c32, 0)[:M], np.concatenate(c16, 0)[:M]


def run_ln(res_tm, mix_tm, g, b):
    nc = _get(("ln", TC), lambda: build_ln(TC))
    gb = np.ascontiguousarray(np.broadcast_to(np.stack([g, b])[None], (128, 2, D))).astype(np.float32)
    out = _run(nc, [{"res": np.ascontiguousarray(res_tm[c * TC:(c + 1) * TC]),
                     "mix": np.ascontiguousarray(mix_tm[c * TC:(c + 1) * TC]), "gb": gb} for c in range(NCORE)])
    return (np.concatenate([np.asarray(o["o32"]) for o in out], 0),
            np.concatenate([np.asarray(o["o16"]) for o in out], 0))


def run_ffnmid(vT, gT, conv_w, conv_b):
    nc = _get(("ffnmid", TC), lambda: build_ffnmid(TC))
    CT = DFF // 128
    cw = np.concatenate([conv_w.T, conv_b[:, None]], axis=1).astype(np.float32)
    cw = np.ascontiguousarray(cw.reshape(CT, 128, 4).transpose(1, 0, 2))
    ins = []
    for c in range(NCORE):
        t0 = c * TC
        gh = np.zeros((DFF, TC + 2), np.float32)
        gh[:, 1:TC + 1] = gT[:, t0:t0 + TC]
        if t0 % SEQ != 0:
            gh[:, 0] = gT[:, t0 - 1]
        if (t0 + TC) % SEQ != 0:
            gh[:, TC + 1] = gT[:, t0 + TC]
        ins.append({"v": np.ascontiguousarray(vT[:, t0:t0 + TC]), "g": gh, "cw": cw})
    out = _run(nc, ins)
    return np.concatenate([np.asarray(o["h"]) for o in out], axis=1)


def run_scan(gateT, upT, conv_w, conv_b, b_a, b_x, lam, wa_b, wx_b):
    NU = 4
    nc = _get(("scan", NU, SEQ), lambda: build_scan(NU, SEQ))
    ins = []
    units = []
    for c in range(NCORE):
        up = np.zeros((NU, 80, SEQ), np.float32)
        ga = np.zeros((NU, 80, SEQ), np.float32)
        pp = np.zeros((NU, 80, 16), np.float32)
        gw = np.zeros((NU, 80, 4, 80), NPBF)
        for j in range(NU):
            b = j // 2
            n = 2 * c + j % 2
            units.append((b, n))
            ch = slice(n * 80, (n + 1) * 80)
            up[j] = upT[ch, b * SEQ:(b + 1) * SEQ]
            ga[j] = gateT[ch, b * SEQ:(b + 1) * SEQ]
            pp[j, :, 0:4] = conv_w[:, ch].T
            pp[j, :, 4] = conv_b[ch]
            pp[j, :, 5:7] = b_a[:, ch].T
            pp[j, :, 7:9] = b_x[:, ch].T
            pp[j, :, 9:11] = lam[:, ch].T
            for d in range(2):
                gw[j, :, 2 * d + 0, :] = wa_b[d, n]
                gw[j, :, 2 * d + 1, :] = wx_b[d, n]
        ins.append({"up": up, "gate": ga, "pp": pp, "gw": gw})
    out = _run(nc, ins)
    yT = np.zeros((DR, NT), NPBF)
    for c in range(NCORE):
        yc = np.asarray(out[c]["y"])
        for j in range(NU):
            b, n = units[c * NU + j]
            yT[n * 80:(n + 1) * 80, b * SEQ:(b + 1) * SEQ] = yc[j]
    return yT


ATTN_GROUPS = ((128, 1), (512, 4), (2048, 16))


def _attn_tables():
    ql = np.arange(128)[:, None]
    kl = np.arange(256)[None, :]
    rel = kl - 64 - ql
    base = np.where(np.abs(rel) <= 64, -np.abs(rel).astype(np.float32), np.float32(NEGBIG))
    first = np.where(kl < 64, np.float32(NEGBIG), base)
    last = np.where(kl >= 192, np.float32(NEGBIG), base)
    return np.ascontiguousarray(np.stack([base, first, last], axis=1).astype(np.float32))


def run_attn(qkv16, S=SEQ, nb=NB):
    slopes = np.exp2(-8.0 * np.arange(1, 17, dtype=np.float64) / 16)
    ncore = NCORE
    HPC = 16 * nb // ncore
    perms = []
    for (win, dil) in ATTN_GROUPS:
        L = S // dil
        perms.append((np.arange(S).reshape(L, dil).T).reshape(-1))
    tab = _attn_tables()
    ident = np.eye(128, dtype=np.float32).astype(NPBF)
    gh_list = []
    for g, (win, dil) in enumerate(ATTN_GROUPS):
        gh_list += [S // dil] * HPC
    nc = _get(("attn", tuple(gh_list), S), lambda: build_attn(gh_list, S))
    ins = []
    cpb = ncore // nb
    for c in range(ncore):
        b = c // cpb
        h0 = (c % cpb) * HPC
        q = np.zeros((3 * HPC, 64, S), NPBF)
        k = np.zeros((3 * HPC, 64, S + 128), NPBF)
        v = np.zeros((3 * HPC, S + 128, 64), NPBF)
        cf = np.zeros((128, 3 * HPC), np.float32)
        for g, (win, dil) in enumerate(ATTN_GROUPS):
            cols = b * S + perms[g]
            for hh in range(HPC):
                h = h0 + hh
                i = g * HPC + hh
                base = g * 3072 + h * 64
                q[i] = qkv16[base:base + 64][:, cols]
                k[i, :, 64:64 + S] = qkv16[base + 1024:base + 1024 + 64][:, cols]
                v[i, 64:64 + S, :] = qkv16[base + 2048:base + 2048 + 64][:, cols].T
                cf[:, i] = np.float32(8.0 * slopes[h] * dil)
        ins.append({"q": q, "k": k, "v": v, "tab": tab, "ident": ident, "coefs": cf})
    out = _run(nc, ins)
    ntok = nb * S
    num = np.zeros((ntok, 3, 1024), np.float32)
    mm = np.zeros((ntok, 48), np.float32)
    den = np.zeros((ntok, 48), np.float32)
    for c in range(ncore):
        b = c // cpb
        h0 = (c % cpb) * HPC
        on = np.asarray(out[c]["num"])
        om = np.asarray(out[c]["mx"])
        od = np.asarray(out[c]["den"])
        for g in range(3):
            rows = b * S + perms[g]
            for hh in range(HPC):
                h = h0 + hh
                i = g * HPC + hh
                num[rows, g, h * 64:(h + 1) * 64] = on[i]
                mm[rows, g * 16 + h] = om[i].T.reshape(-1)
                den[rows, g * 16 + h] = od[i].T.reshape(-1)
    return num, mm, den


def run_merge(num, mm, den):
    nc = _get(("merge", TC), lambda: build_merge(TC))
    out = _run(nc, [{"num": np.ascontiguousarray(num[c * TC:(c + 1) * TC]),
                     "mm": np.ascontiguousarray(mm[c * TC:(c + 1) * TC]),
                     "den": np.ascontiguousarray(den[c * TC:(c + 1) * TC])} for c in range(NCORE)])
    return np.concatenate([np.asarray(o["o"]) for o in out], 0)


def T_(a):
    return np.ascontiguousarray(a.T)


def post_block(x_tm, mixT32, i, ln_g, ln_b, w_up_b, conv_w, conv_b, w_down_b):
    x1, x1b = run_ln(x_tm, T_(mixT32), ln_g[i, 0], ln_b[i, 0])
    up32, _ = gemm_wide(w_up_b, T_(x1b))
    hT = run_ffnmid(up32[:DFF], up32[DFF:], conv_w, conv_b)
    ffn32, _ = run_gemm(w_down_b, hT)
    x2, x2b = run_ln(x1, T_(ffn32), ln_g[i, 1], ln_b[i, 1])
    return x2, x2b


def kernel(x, ln_g, ln_b, rg_w_in, rg_conv_w, rg_conv_b, rg_w_a, rg_b_a, rg_w_x, rg_b_x,
           rg_lam, rg_w_out, at_w_qkv, at_w_o, ff_w_up, ff_conv_w, ff_conv_b, ff_w_down):
    f = lambda a: np.asarray(a, dtype=np.float32)
    x = f(x).reshape(NT, D)
    parts = [T_(x), f(rg_w_in[0]), f(rg_w_out[0]), f(ff_w_up[0]), f(ff_w_up[1]), f(ff_w_down[0]), f(ff_w_down[1]),
             f(at_w_qkv[0]), f(at_w_o[0]), f(rg_w_a[0]), f(rg_w_x[0])]
    flat = np.concatenate([p.reshape(-1) for p in parts])
    flat16 = run_cast(flat)
    outs = []
    o = 0
    for p in parts:
        outs.append(flat16[o:o + p.size].reshape(p.shape))
        o += p.size
    xT16, w_in_b, w_out_b, w_up0, w_up1, w_dn0, w_dn1, w_qkv_b, w_o_b, wa_b, wx_b = outs
    ln_g, ln_b = f(ln_g), f(ln_b)
    c32, _ = gemm_wide(w_in_b, xT16)
    yT = run_scan(c32[:DR], c32[DR:], f(rg_conv_w[0]), f(rg_conv_b[0]), f(rg_b_a[0]), f(rg_b_x[0]), f(rg_lam[0]),
                  wa_b, wx_b)
    mix32, _ = run_gemm(pad_rows(w_out_b, DFF), pad_rows(yT, DFF))
    x2, x2b = post_block(x, mix32, 0, ln_g, ln_b, w_up0, f(ff_conv_w[0]), f(ff_conv_b[0]), w_dn0)
    _, qkv16 = gemm_wide(w_qkv_b, T_(x2b))
    num, mm, den = run_attn(qkv16)
    o16 = run_merge(num, mm, den)
    mix32, _ = gemm_wide(w_o_b, T_(o16))
    x4, _ = post_block(x2, mix32, 1, ln_g, ln_b, w_up1, f(ff_conv_w[1]), f(ff_conv_b[1]), w_dn1)
    return x4.reshape(NB, SEQ, D).astype(np.float32)
```
